# Optimizing a Trainium2 kernel written in Bass

```python
import math
import jax, jax.numpy as jnp
from jax import lax
import numpy as np

D_MODEL = 1024
BATCH = 1
SEQ = 16384
DEPTH = 4

CHUNK = 64
QBLK = 128
POOL_WIDTH = D_MODEL
POOL_GROUPS = 4
POOL_GROUP = POOL_WIDTH // POOL_GROUPS
POOL_WINDOWS = (2, 4, 8, 16)
DIFF_QK_DIM = 128
DIFF_V_DIM = 2 * DIFF_QK_DIM
DIFF_HEADS = D_MODEL // DIFF_V_DIM
DIFF_QK_WIDTH = DIFF_HEADS * 2 * DIFF_QK_DIM
DIFF_V_WIDTH = DIFF_HEADS * DIFF_V_DIM
ROT_DIM = DIFF_QK_DIM // 4
ROPE_THETA = 500000.0
N_BRANCHES = 2
IN_WIDTH = POOL_WIDTH + 2 * DIFF_QK_WIDTH + DIFF_V_WIDTH + N_BRANCHES * D_MODEL
FFN_HIDDEN = -(-8 * D_MODEL // (3 * 256)) * 256
NORM_EPS = 1e-6

kernel_name = "hybrid_pool_diffattn_gated_block"


def rmsnorm(x, g):
    xf = x.astype(jnp.float32)
    y = xf * lax.rsqrt(jnp.mean(xf * xf, axis=-1, keepdims=True) + NORM_EPS)
    return (y * g.astype(jnp.float32)).astype(x.dtype)


def rotary_tables(seq, dtype):
    pos = jnp.arange(seq, dtype=jnp.float32)
    inv_freq = ROPE_THETA ** (-jnp.arange(0, ROT_DIM, 2, dtype=jnp.float32) / ROT_DIM)
    ang = pos[:, None] * inv_freq[None, :]
    return jnp.cos(ang).astype(dtype), jnp.sin(ang).astype(dtype)


def apply_partial_rotary(t, cos, sin):
    half = ROT_DIM // 2
    t1, t2, rest = t[..., :half], t[..., half:ROT_DIM], t[..., ROT_DIM:]
    return jnp.concatenate([t1 * cos - t2 * sin, t2 * cos + t1 * sin, rest], axis=-1)


def pool_mixer(u, w_pool, scale):
    b, s, _ = u.shape
    ug = u.reshape(b, s, POOL_GROUPS, POOL_GROUP)
    t_idx = jnp.arange(s, dtype=jnp.float32)[None, :, None]
    outs = []
    for gi, w in enumerate(POOL_WINDOWS):
        xg = ug[:, :, gi, :].astype(jnp.float32)
        c = jnp.cumsum(xg, axis=1)
        c_shift = jnp.pad(c, ((0, 0), (w, 0), (0, 0)))[:, :s]
        count = jnp.minimum(t_idx + 1.0, float(w))
        outs.append((c - c_shift) / count - xg)
    pooled = jnp.stack(outs, axis=2).astype(u.dtype)
    y = jnp.einsum('bsgc,gcd->bsgd', pooled, w_pool)
    return y.reshape(b, s, POOL_WIDTH) * scale


def diff_attention(q, k, v, lam):
    b, h, _, s, d = q.shape
    nblk = s // QBLK
    chunk_ids = jnp.arange(s) // CHUNK
    qb = q.reshape(b, h, 2, nblk, QBLK, d).transpose(3, 0, 1, 2, 4, 5)
    qc = chunk_ids.reshape(nblk, QBLK)
    scale = DIFF_QK_DIM ** -0.5
    neg = jnp.finfo(jnp.float32).min

    def one_block(args):
        qi, ci = args
        sc = jnp.einsum('bhcqd,bhckd->bhcqk', qi, k).astype(jnp.float32) * scale
        mask = ci[:, None] >= chunk_ids[None, :]
        p = jax.nn.softmax(jnp.where(mask, sc, neg), axis=-1)
        a = p[:, :, 0] - lam * p[:, :, 1]
        return jnp.einsum('bhqk,bhkv->bhqv', a.astype(v.dtype), v)

    out = lax.map(one_block, (qb, qc))
    return out.transpose(1, 2, 0, 3, 4).reshape(b, h, s, DIFF_V_DIM)


def setup_inputs(seed: int = 0) -> dict:
    key = jax.random.key(seed)
    ks = jax.random.split(key, 16)
    f32 = jnp.float32
    nrm = lambda k, shp, std: jax.random.normal(k, shp, f32) * std
    return {
        "x": jax.random.normal(ks[0], (BATCH, SEQ, D_MODEL), f32),
        "norm1_g": 1.0 + nrm(ks[1], (DEPTH, D_MODEL), 0.02),
        "w_in": nrm(ks[2], (DEPTH, D_MODEL, IN_WIDTH), D_MODEL ** -0.5),
        "q_norm_g": 1.0 + nrm(ks[3], (DEPTH, DIFF_QK_DIM), 0.02),
        "k_norm_g": 1.0 + nrm(ks[4], (DEPTH, DIFF_QK_DIM), 0.02),
        "lam_q1": nrm(ks[5], (DEPTH, DIFF_QK_DIM), 0.1),
        "lam_k1": nrm(ks[6], (DEPTH, DIFF_QK_DIM), 0.1),
        "lam_q2": nrm(ks[7], (DEPTH, DIFF_QK_DIM), 0.1),
        "lam_k2": nrm(ks[8], (DEPTH, DIFF_QK_DIM), 0.1),
        "subln_g": 1.0 + nrm(ks[9], (DEPTH, DIFF_V_DIM), 0.02),
        "pool_w": nrm(ks[10], (DEPTH, POOL_GROUPS, POOL_GROUP, POOL_GROUP), POOL_GROUP ** -0.5),
        "pool_scale": 1.0 + nrm(ks[11], (DEPTH, POOL_WIDTH), 0.02),
        "w_out": nrm(ks[12], (DEPTH, D_MODEL, D_MODEL), D_MODEL ** -0.5),
        "norm2_g": 1.0 + nrm(ks[13], (DEPTH, D_MODEL), 0.02),
        "w_ffn_in": nrm(ks[14], (DEPTH, D_MODEL, 2 * FFN_HIDDEN), D_MODEL ** -0.5),
        "w_ffn_out": nrm(ks[15], (DEPTH, FFN_HIDDEN, D_MODEL), FFN_HIDDEN ** -0.5),
    }


def reference(x, norm1_g, w_in, q_norm_g, k_norm_g, lam_q1, lam_k1, lam_q2, lam_k2,
              subln_g, pool_w, pool_scale, w_out, norm2_g, w_ffn_in, w_ffn_out):
    b, s, _ = x.shape
    cos, sin = rotary_tables(s, x.dtype)
    splits = np.cumsum([POOL_WIDTH, DIFF_QK_WIDTH, DIFF_QK_WIDTH, DIFF_V_WIDTH, D_MODEL]).tolist()
    for i in range(DEPTH):
        lam_init = 0.8 - 0.6 * math.exp(-0.3 * i)
        xn = rmsnorm(x, norm1_g[i])
        proj = xn @ w_in[i]
        u_pool, q, k, v, g_a, g_b = jnp.split(proj, splits, axis=-1)
        q = q.reshape(b, s, DIFF_HEADS, 2, DIFF_QK_DIM).transpose(0, 2, 3, 1, 4)
        k = k.reshape(b, s, DIFF_HEADS, 2, DIFF_QK_DIM).transpose(0, 2, 3, 1, 4)
        q = apply_partial_rotary(rmsnorm(q, q_norm_g[i]), cos, sin)
        k = apply_partial_rotary(rmsnorm(k, k_norm_g[i]), cos, sin)
        v = v.reshape(b, s, DIFF_HEADS, DIFF_V_DIM).transpose(0, 2, 1, 3)
        lam = (jnp.exp(jnp.sum(lam_q1[i].astype(jnp.float32) * lam_k1[i].astype(jnp.float32)))
               - jnp.exp(jnp.sum(lam_q2[i].astype(jnp.float32) * lam_k2[i].astype(jnp.float32)))
               + lam_init)
        o = diff_attention(q, k, v, lam)
        o = rmsnorm(o, subln_g[i]) * (1.0 - lam_init)
        attn_out = o.transpose(0, 2, 1, 3).reshape(b, s, DIFF_V_WIDTH)
        pool_out = pool_mixer(u_pool, pool_w[i], pool_scale[i])
        merged = jax.nn.sigmoid(g_a) * pool_out + jax.nn.sigmoid(g_b) * attn_out
        x = x + merged @ w_out[i]
        hn = rmsnorm(x, norm2_g[i])
        gate, up = jnp.split(hn @ w_ffn_in[i], 2, axis=-1)
        x = x + (jax.nn.silu(gate) * up) @ w_ffn_out[i]
    return x
```

```python
import math
from contextlib import ExitStack

import numpy as np
import ml_dtypes

import concourse.bass as bass
import concourse.mybir as mybir
from concourse.bass_utils import run_bass_kernel_spmd

F32 = mybir.dt.float32
BF16 = mybir.dt.bfloat16
AF = mybir.ActivationFunctionType
ALU = mybir.AluOpType
AX = mybir.AxisListType

NCORES = 8
D = 1024
SEQ = 16384
DEPTH = 4
TPC = SEQ // NCORES
NB = TPC // 128
INW = 6144
FH = 2816
NFC = FH // 128
EPS = 1e-6
ROT = 32
HALF = 16
SCALE = 128 ** -0.5
POOL_WINDOWS = (2, 4, 8, 16)
VW = 257


class Q:
    def __init__(self, nc, es, eng, name):
        self.eng = eng
        self.name = name
        self.sem = es.enter_context(nc.semaphore("sem_" + name))
        self.n = 0
        self.seen = {}

    def wait(self, *toks):
        for t in toks:
            if t is None:
                continue
            if isinstance(t, (list,)):
                self.wait(*t)
                continue
            sem, c, key = t
            if self.seen.get(key, 0) >= c:
                continue
            self.eng.wait_ge(sem, c)
            self.seen[key] = c

    def sig(self, inst):
        self.n += 1
        inst.then_inc(self.sem, 1)
        return (self.sem, self.n, self.name)


class Slot:
    def __init__(self, nc, es, name):
        self.sem = es.enter_context(nc.semaphore("dsem_" + name))
        self.name = "d_" + name
        self.n = 0

    def sig(self, inst):
        self.n += 16
        inst.then_inc(self.sem, 16)
        return (self.sem, self.n, self.name)


class Ctx:
    def __init__(self, nc, es):
        self.nc = nc
        self.es = es
        self.pe = Q(nc, es, nc.tensor, "pe")
        self.act = Q(nc, es, nc.scalar, "act")
        self.dve = Q(nc, es, nc.vector, "dve")
        self.pool = Q(nc, es, nc.gpsimd, "pool")
        self.sp = Q(nc, es, nc.sync, "sp")
        self._slots = {}
        self.ps = [es.enter_context(nc.psum_tensor(f"psb{i}", [128, 512], F32)) for i in range(8)]

    def slot(self, name):
        if name not in self._slots:
            self._slots[name] = Slot(self.nc, self.es, name)
        return self._slots[name]

    def sb(self, name, shape, dt):
        return self.es.enter_context(self.nc.sbuf_tensor("s_" + name, list(shape), dt))

    def dma(self, q, slotname, out, in_, waits=()):
        q.wait(*waits)
        inst = q.eng.dma_start(out=out, in_=in_)
        return self.slot(slotname).sig(inst)


def bcast_rows(ap1d, n=128):
    return ap1d.partition_broadcast(n)


def op(q, fn, waits=(), sig=True):
    q.wait(*waits)
    inst = fn(q.eng)
    return q.sig(inst) if sig else None


def bcast_rows(ap1d, n=128):
    return ap1d.partition_broadcast(n)


def rms_rstd(c, ss_ap, n, nh_ap, out_ap, waits):
    t = op(c.dve, lambda e: e.tensor_scalar(out=ss_ap, in0=ss_ap, scalar1=1.0 / n, scalar2=EPS,
                                            op0=ALU.mult, op1=ALU.add), waits)
    return op(c.pool, lambda e: e.tensor_tensor(out=out_ap, in0=ss_ap, in1=nh_ap, op=ALU.pow), [t])


def norm_to_T(c, S, src_ap, gb, t_g, dstT, tb, waits, st):
    dve, pe, act = c.dve, c.pe, c.act
    j = st["i"] % 2
    st["i"] += 1
    sq, ss, rs, xnb = S["sq"], S["ss"], S["rs"], S["xnb"]
    t = op(dve, lambda e: e.tensor_tensor(out=sq[:, j, :], in0=src_ap, in1=src_ap, op=ALU.mult), list(waits) + [st["sqf"][j]])
    t = op(dve, lambda e: e.tensor_reduce(out=ss[:, j:j + 1], in_=sq[:, j, :], axis=AX.X, op=ALU.add), [t])
    st["sqf"][j] = t
    t_rs = rms_rstd(c, ss[:, j:j + 1], D, S["neghalf"][:, 0:1], rs[:, j:j + 1], [t, S["const_ready"]])
    t_xn = op(dve, lambda e: e.scalar_tensor_tensor(out=xnb[:, j, :], in0=src_ap, scalar=rs[:, j:j + 1], in1=gb[:],
                                                    op0=ALU.mult, op1=ALU.mult), [t_rs, t_g, st["xnbf"][j]])
    tpb = c.ps[6 + j][:].bitcast(BF16)
    pe.wait(t_xn, st["tpf"][j], S["const_ready"])
    for kc in range(8):
        ins = pe.eng.transpose(out=tpb[:, kc * 128:(kc + 1) * 128], in_=xnb[:, j, kc * 128:(kc + 1) * 128],
                               identity=S["ident"][:])
    t_tp = pe.sig(ins)
    st["xnbf"][j] = t_tp
    t_cp = op(act, lambda e: e.activation(out=dstT[:, :, tb * 128:(tb + 1) * 128],
                                          in_=tpb.rearrange("p (k t) -> p k t", k=8), func=AF.Copy), [t_tp])
    st["tpf"][j] = t_cp
    return t_cp


def new_norm_state():
    return dict(i=0, sqf=[None, None], xnbf=[None, None], tpf=[None, None])


def phase_A(c, X, x_ready, W, O, S):
    pe, act, dve, pool, sp = c.pe, c.act, c.dve, c.pool, c.sp
    out_toks = []
    g1b, gqb, gkb = S["g1b"], S["gqb"], S["gkb"]
    t_g1 = c.dma(sp, "cA0", g1b[:], bcast_rows(W["norm1_g"]))
    t_gq = c.dma(sp, "cA1", gqb[:], bcast_rows(W["q_norm_g"]))
    t_gk = c.dma(sp, "cA2", gkb[:], bcast_rows(W["k_norm_g"]))
    xnT = S["xnT"]
    nh = S["neghalf"]

    wbuf = S["wbuf"]
    w_free = [None, None, None]
    w_in = W["w_in"].rearrange("(k p) n -> p k n", p=128)
    NG = INW // 512
    w_tok = [None] * NG

    def load_w(g):
        j = g % 3
        w_tok[g] = c.dma(pool, f"wA{j}", wbuf[:, j, :, :], w_in[:, :, g * 512:(g + 1) * 512], waits=[w_free[j]])

    load_w(0)
    load_w(1)

    nst = new_norm_state()
    nst["tpf"] = S.get("tpf0", [None, None])
    xr = x_ready if isinstance(x_ready, list) else [x_ready] * NB
    xnT_tok = [None] * NB
    for tb in range(NB):
        xnT_tok[tb] = norm_to_T(c, S, X[:, tb, :], g1b, t_g1, xnT, tb, [xr[tb]], nst)
    tp_free = nst["tpf"]

    pbank_free = [None] * 4
    stg, vst, qn, qb, qts = S["stg"], S["vst"], S["qn"], S["qb"], S["qts"]
    stg_free = [None] * 4
    vst_free = [None, None]
    qn_free = [None, None]
    qb_free = [None, None]
    qts_free = [None, None]
    qss, qrs = S["qss"], S["qrs"]
    cosb, sinb = S["cos"], S["sin"]
    rt = S["rt"]
    it = 0
    qk_it = 0
    pending = []

    def qk_stage2(args):
        (kind, g, tb, pb, qj, t3, n0) = args
        bank = c.ps[pb]
        gb_ = gqb if kind == "q" else gkb
        t_gb = t_gq if kind == "q" else t_gk
        hc0 = ((n0 - 1024) % 1024) // 128
        b3 = bank[:].rearrange("p (h d) -> p h d", h=4)
        dve.wait(t3, t_gb)
        for h in range(4):
            ins = dve.eng.scalar_tensor_tensor(out=qn[:, qj, h * 128:(h + 1) * 128], in0=b3[:, h, :],
                                               scalar=qrs[:, qj, h:h + 1], in1=gb_[:], op0=ALU.mult, op1=ALU.mult)
        t_qn = dve.sig(ins)
        pbank_free[pb] = t_qn
        q3 = qn[:, qj, :].rearrange("p (h d) -> p h d", h=4)
        o3 = qb[:, qj, :].rearrange("p (h d) -> p h d", h=4)
        cb = cosb[:, tb:tb + 1, :].broadcast_to([128, 4, HALF])
        sb_ = sinb[:, tb:tb + 1, :].broadcast_to([128, 4, HALF])
        pool.wait(t_qn, qb_free[qj], S["const_ready"])
        pool.eng.tensor_tensor(out=rt[:, qj, 0, :, :], in0=q3[:, :, 0:HALF], in1=cb, op=ALU.mult)
        pool.eng.tensor_tensor(out=rt[:, qj, 1, :, :], in0=q3[:, :, HALF:ROT], in1=sb_, op=ALU.mult)
        pool.eng.tensor_tensor(out=rt[:, qj, 2, :, :], in0=q3[:, :, HALF:ROT], in1=cb, op=ALU.mult)
        tr = pool.sig(pool.eng.tensor_tensor(out=rt[:, qj, 3, :, :], in0=q3[:, :, 0:HALF], in1=sb_, op=ALU.mult))
        pool.wait(tr)
        pool.eng.tensor_tensor(out=o3[:, :, 0:HALF], in0=rt[:, qj, 0, :, :], in1=rt[:, qj, 1, :, :], op=ALU.subtract)
        pool.eng.tensor_tensor(out=o3[:, :, HALF:ROT], in0=rt[:, qj, 2, :, :], in1=rt[:, qj, 3, :, :], op=ALU.add)
        t_qb = pool.sig(pool.eng.tensor_copy(out=o3[:, :, ROT:128], in_=q3[:, :, ROT:128]))
        qn_free[qj] = t_qb
        tj = qj
        tpb = c.ps[6 + tj][:].bitcast(BF16)
        pe.wait(t_qb, tp_free[tj], S["const_ready"])
        for h in range(4):
            ins = pe.eng.transpose(out=tpb[:, h * 128:(h + 1) * 128], in_=qb[:, qj, h * 128:(h + 1) * 128],
                                   identity=S["ident"][:])
        t_tp = pe.sig(ins)
        qb_free[qj] = t_tp
        sblk = tb // 4
        sj2 = (g * 4 + sblk) % 2
        t_cp = op(act, lambda e: e.activation(out=qts[:, sj2, :, (tb % 4) * 128:(tb % 4 + 1) * 128],
                                              in_=tpb[:, 0:512].rearrange("p (h t) -> p h t", h=4), func=AF.Copy),
                  [t_tp, qts_free[sj2] if tb % 4 == 0 else None])
        tp_free[tj] = t_cp
        if tb % 4 == 3:
            dst = O["QT"] if kind == "q" else O["KT"]
            t_st = c.dma(sp, f"qts{sj2}", dst[hc0:hc0 + 4, :, sblk * 512:(sblk + 1) * 512].rearrange("h d t -> d h t"),
                         qts[:, sj2, :, :], waits=[t_cp])
            qts_free[sj2] = t_st
            out_toks.append(t_st)

    for g in range(NG):
        if g + 2 < NG:
            load_w(g + 2)
        n0 = g * 512
        kind = ["u", "q", "k", "v", "ga", "gb"][n0 // 1024]
        last_pe = None
        for tb in range(NB):
            pb = it % 4
            bank = c.ps[pb]
            pe.wait(xnT_tok[tb], w_tok[g], pbank_free[pb])
            for kc in range(8):
                ins = pe.eng.matmul(bank[:], lhsT=xnT[:, kc, tb * 128:(tb + 1) * 128], rhs=wbuf[:, g % 3, kc, :],
                                    start=(kc == 0), stop=(kc == 7))
            t_mm = pe.sig(ins)
            last_pe = t_mm
            if kind in ("u", "ga", "gb"):
                sj = it % 4
                dst = {"u": O["U"], "ga": O["sgA"], "gb": O["sgB"]}[kind]
                col = n0 % 1024
                fn = AF.Copy if kind == "u" else AF.Sigmoid
                t_e = op(act, lambda e: e.activation(out=stg[:, sj, :], in_=bank[:], func=fn), [t_mm, stg_free[sj]])
                pbank_free[pb] = t_e
                t_st = c.dma(sp, f"stg{sj}", dst[tb * 128:(tb + 1) * 128, col:col + 512], stg[:, sj, :], waits=[t_e])
                stg_free[sj] = t_st
                out_toks.append(t_st)
            elif kind == "v":
                vj = it % 2
                h0 = (n0 - 3072) // 256
                t_e = op(dve, lambda e: e.tensor_copy(out=vst[:, vj, :, 0:256],
                                                      in_=bank[:].rearrange("p (h v) -> p h v", h=2)),
                         [t_mm, vst_free[vj], S["const_ready"]])
                pbank_free[pb] = t_e
                t_st = c.dma(sp, f"vst{vj}", O["V"][h0:h0 + 2, tb * 128:(tb + 1) * 128, :].rearrange("h p v -> p h v"),
                             vst[:, vj, :, :], waits=[t_e])
                vst_free[vj] = t_st
                out_toks.append(t_st)
            else:
                qj = qk_it % 2
                qk_it += 1
                if pending:
                    qk_stage2(pending.pop(0))
                t = op(act, lambda e: e.activation(out=qn[:, qj, :], in_=bank[:], func=AF.Square),
                       [t_mm, qn_free[qj]])
                t = op(dve, lambda e: e.tensor_reduce(out=qss[:, qj, :], in_=qn[:, qj, :].rearrange("p (h d) -> p h d", h=4),
                                                      axis=AX.X, op=ALU.add), [t])
                t3 = rms_rstd(c, qss[:, qj, :], 128, nh[:, 0:4], qrs[:, qj, :], [t, S["const_ready"]])
                pending.append((kind, g, tb, pb, qj, t3, n0))
            it += 1
        w_free[g % 3] = last_pe
        if kind == "k" and n0 == 2560:
            while pending:
                qk_stage2(pending.pop(0))
    while pending:
        qk_stage2(pending.pop(0))
    S["tpf0"] = tp_free
    return out_toks


def alloc_A(c):
    S = {}
    S["g1b"] = c.sb("g1b", [128, D], F32)
    S["gqb"] = c.sb("gqb", [128, 128], F32)
    S["gkb"] = c.sb("gkb", [128, 128], F32)
    S["sq"] = c.sb("sq", [128, 2, D], F32)
    S["ss"] = c.sb("ss", [128, 2], F32)
    S["rs"] = c.sb("rs", [128, 2], F32)
    S["xnb"] = c.sb("xnb", [128, 2, D], BF16)
    S["xnT"] = c.sb("xnT", [128, 8, TPC], BF16)
    S["wbuf"] = c.sb("wbuf", [128, 3, 8, 512], BF16)
    S["stg"] = c.sb("stg", [128, 4, 512], BF16)
    S["vst"] = c.sb("vst", [128, 2, 2, VW], BF16)
    S["qn"] = c.sb("qn", [128, 2, 512], F32)
    S["qb"] = c.sb("qb", [128, 2, 512], BF16)
    S["qts"] = c.sb("qts", [128, 2, 4, 512], BF16)
    S["qss"] = c.sb("qss", [128, 2, 4], F32)
    S["qrs"] = c.sb("qrs", [128, 2, 4], F32)
    S["rt"] = c.sb("rt", [128, 2, 4, 4, HALF], F32)
    return S


def alloc_common(c, K):
    S = {}
    S["ident"] = c.sb("ident", [128, 128], BF16)
    S["neghalf"] = c.sb("neghalf", [128, 16], F32)
    S["cos"] = c.sb("cosb", [128, NB, HALF], F32)
    S["sin"] = c.sb("sinb", [128, NB, HALF], F32)
    toks = []
    toks.append(c.dma(c.sp, "k0", S["ident"][:], K["ident"]))
    toks.append(c.dma(c.sp, "k1", S["cos"][:], K["cos"].rearrange("(b p) r -> p b r", p=128)))
    toks.append(c.dma(c.sp, "k2", S["sin"][:], K["sin"].rearrange("(b p) r -> p b r", p=128)))
    toks.append(c.pool.sig(c.pool.eng.memset(S["neghalf"][:], -0.5)))
    S["const_ready"] = toks
    return S


def load_X(c, X, x):
    xv = x.rearrange("(b p) d -> p b d", p=128)
    toks = []
    for i in range(4):
        t = c.dma(c.sp, f"xl{i}", X[:, 4 * i:4 * i + 4, :], xv[:, 4 * i:4 * i + 4, :])
        toks += [t] * 4
    return toks


def dram(nc, name, shape, d, kind):
    return nc.dram_tensor(name, list(shape), d, kind=kind).ap()


def build_A():
    nc = bass.Bass("TRN2", target_bir_lowering=False)
    x = dram(nc, "x", [TPC, D], F32, "ExternalInput")
    W = dict(norm1_g=dram(nc, "norm1_g", [D], F32, "ExternalInput"),
             q_norm_g=dram(nc, "q_norm_g", [128], F32, "ExternalInput"),
             k_norm_g=dram(nc, "k_norm_g", [128], F32, "ExternalInput"),
             w_in=dram(nc, "w_in", [D, INW], F32, "ExternalInput"))
    K = dict(ident=dram(nc, "ident", [128, 128], BF16, "ExternalInput"),
             cos=dram(nc, "cos", [TPC, HALF], F32, "ExternalInput"),
             sin=dram(nc, "sin", [TPC, HALF], F32, "ExternalInput"))
    O = dict(U=dram(nc, "U", [TPC, D], BF16, "ExternalOutput"),
             sgA=dram(nc, "sgA", [TPC, D], BF16, "ExternalOutput"),
             sgB=dram(nc, "sgB", [TPC, D], BF16, "ExternalOutput"),
             V=dram(nc, "V", [4, TPC, VW], BF16, "ExternalOutput"),
             QT=dram(nc, "QT", [8, 128, TPC], BF16, "ExternalOutput"),
             KT=dram(nc, "KT", [8, 128, TPC], BF16, "ExternalOutput"))
    with ExitStack() as es:
        c = Ctx(nc, es)
        es.enter_context(nc.Block())
        S = alloc_common(c, K)
        S.update(alloc_A(c))
        X = c.sb("X", [128, NB, D], F32)
        xr = load_X(c, X, x)
        t = c.dve.sig(c.dve.eng.memset(S["vst"][:, :, :, 256:257], 1.0))
        S["const_ready"] = S["const_ready"] + [t]
        toks = phase_A(c, X, xr, W, O, S)
        c.sp.wait(*toks)
    return nc


def phase_B(c, I, O, S, in_ready=None, LA=2):
    pe, act, dve, pool, sp = c.pe, c.act, c.dve, c.pool, c.sp
    KTs, qt, vp, PT, ost, rec = S["KT"], S["qt"], S["vp"], S["PT"], S["ost"], S["rec"]
    NT = SEQ // 512
    out_toks = []
    kt_tok = []
    for r in range(8):
        kt_tok.append(c.dma(sp, f"kt{r}", KTs[:, r * 2048:(r + 1) * 2048], I["KT"][:, r * 2048:(r + 1) * 2048],
                            waits=[in_ready]))
    units = [(t, kb) for t in range(NT) for kb in range(4 * t + 4)]
    NU = len(units)
    pieces = [(t, p) for t in range(NT) for p in range(t + 1)]
    piece_of = {}
    for i, (t, p) in enumerate(pieces):
        for kb in range(4 * p, 4 * p + 4):
            piece_of[(t, kb)] = i
    NVS = 4
    v_tok = [None] * len(pieces)
    v_free = [None] * NVS
    piece_last_pv = [None] * len(pieces)
    next_v = [0]
    Vv = I["V"].rearrange("(j p) v -> p j v", p=128)

    def load_v_upto(i_max):
        while next_v[0] <= min(i_max, len(pieces) - 1):
            i = next_v[0]
            t, p = pieces[i]
            sl = i % NVS
            if i >= NVS:
                assert piece_last_pv[i - NVS] is not None
            v_tok[i] = c.dma(sp, f"vp{sl}", vp[:, sl, :, :], Vv[:, 4 * p:4 * p + 4, :],
                             waits=[in_ready, piece_last_pv[i - NVS] if i >= NVS else None])
            next_v[0] += 1

    q_tok = [None] * NT
    q_last_qk = [None] * NT
    next_q = [0]

    def load_q_upto(tmax):
        while next_q[0] <= min(tmax, NT - 1):
            t = next_q[0]
            if t >= 3:
                assert q_last_qk[t - 3] is not None
            q_tok[t] = c.dma(sp, f"qt{t % 3}", qt[:, t % 3, :], I["QT"][:, t * 512:(t + 1) * 512],
                             waits=[in_ready, q_last_qk[t - 3] if t >= 3 else None])
            next_q[0] += 1

    st_free = [None] * 4
    pt_free = [None] * 4
    exp_tok = [None] * NU
    acc_free = None
    ost_free = [None, None]
    last_pv_of_tile = {}

    def emit_qk(u):
        t, kb = units[u]
        i = kb - 4 * t
        c0 = max(i, 0) * 128
        load_q_upto(t + 1 if q_last_qk[max(t - 2, 0)] is not None or t < 2 else t)
        load_v_upto(piece_of[(t, kb)] + 1)
        bank = c.ps[4 + u % 4]
        pe.wait(q_tok[t], kt_tok[kb // 16], st_free[u % 4])
        ins = pe.eng.matmul(bank[:, c0:512], lhsT=KTs[:, kb * 128:(kb + 1) * 128], rhs=qt[:, t % 3, c0:512],
                            start=True, stop=True)
        t_qk = pe.sig(ins)
        q_last_qk[t] = t_qk
        t_e = op(act, lambda e: e.activation(out=PT[:, u % 4, c0:512], in_=bank[:, c0:512], func=AF.Exp, scale=SCALE),
                 [t_qk, pt_free[u % 4]])
        st_free[u % 4] = t_e
        if i >= 0:
            t_e = op(pool, lambda e: e.memset(PT[64:128, u % 4, c0:c0 + 64], 0.0), [t_e])
        exp_tok[u] = t_e

    def emit_pv(u):
        nonlocal acc_free
        t, kb = units[u]
        i = kb - 4 * t
        q0 = max(i, 0)
        pi = piece_of[(t, kb)]
        pe.wait(exp_tok[u], v_tok[pi], acc_free if kb == 0 else None)
        for qs in range(q0, 4):
            ins = pe.eng.matmul(c.ps[qs][:, 0:VW], lhsT=PT[:, u % 4, qs * 128:(qs + 1) * 128],
                                rhs=vp[:, pi % NVS, kb % 4, :], start=(kb == 0), stop=(kb == 4 * t + qs))
        t_pv = pe.sig(ins)
        pt_free[u % 4] = t_pv
        piece_last_pv[pi] = t_pv
        if kb == 4 * t + 3:
            finalize(t, t_pv)

    def finalize(t, t_pv):
        nonlocal acc_free
        sl = t % 2
        dve.wait(t_pv, ost_free[sl])
        for qs in range(4):
            ins = dve.eng.reciprocal(out=rec[:, sl, qs:qs + 1], in_=c.ps[qs][:, 256:257])
        t_r = dve.sig(ins)
        dve.wait(t_r)
        for qs in range(4):
            ins = dve.eng.tensor_scalar(out=ost[:, sl, qs, :], in0=c.ps[qs][:, 0:256], scalar1=rec[:, sl, qs:qs + 1],
                                        scalar2=None, op0=ALU.mult)
        t_f = dve.sig(ins)
        acc_free = t_f
        t_st = c.dma(sp, f"ost{sl}", O["O"][t * 512:(t + 1) * 512, :].rearrange("(q p) v -> p q v", p=128),
                     ost[:, sl, :, :], waits=[t_f])
        ost_free[sl] = t_st
        out_toks.append(t_st)

    for u in range(NU + LA):
        if u < NU:
            emit_qk(u)
        if u - LA >= 0:
            emit_pv(u - LA)
    return out_toks


def alloc_B(c):
    S = {}
    S["KT"] = c.sb("KTs", [128, SEQ], BF16)
    S["qt"] = c.sb("qt", [128, 3, 512], BF16)
    S["vp"] = c.sb("vp", [128, 4, 4, VW], BF16)
    S["PT"] = c.sb("PT", [128, 4, 512], BF16)
    S["ost"] = c.sb("ost", [128, 2, 4, 256], F32)
    S["rec"] = c.sb("rec", [128, 2, 4], F32)
    return S


def build_B():
    nc = bass.Bass("TRN2", target_bir_lowering=False)
    I = dict(QT=dram(nc, "QT", [128, SEQ], BF16, "ExternalInput"),
             KT=dram(nc, "KT", [128, SEQ], BF16, "ExternalInput"),
             V=dram(nc, "V", [SEQ, VW], BF16, "ExternalInput"))
    O = dict(O=dram(nc, "O", [SEQ, 256], F32, "ExternalOutput"))
    with ExitStack() as es:
        c = Ctx(nc, es)
        es.enter_context(nc.Block())
        S = alloc_B(c)
        toks = phase_B(c, I, O, S)
        c.sp.wait(*toks)
    return nc


def phase_C1(c, X, W, I, S, in_ready=None, core0_special=True):
    pe, act, dve, pool, sp = c.pe, c.act, c.dve, c.pool, c.sp
    nh = S["neghalf"]
    g2b, subg, psb, lcb, lv = S["g2b"], S["subg"], S["psb"], S["lcb"], S["lv"]
    wout, pwf, pw = S["wout"], S["pwf"], S["pw"]
    bands = S["bands"]
    t_g2 = c.dma(sp, "cC0", g2b[:], bcast_rows(W["norm2_g"]))
    t_sg = c.dma(sp, "cC1", subg[:], bcast_rows(W["subln_g"]))
    t_ps = c.dma(sp, "cC2", psb[:], bcast_rows(W["pool_scale"]))
    t_lc = c.dma(sp, "cC3", lcb[:], bcast_rows(W["lc"]))
    t_lv = []
    for i, nm in enumerate(["lam_q1", "lam_k1", "lam_q2", "lam_k2"]):
        t_lv.append(c.dma(sp, f"cC4{i}", lv[:, i, :], bcast_rows(W[nm])))
    t_pw = c.dma(sp, "cC5", pwf[:], W["pool_w"].rearrange("g (cc p) d -> p g cc d", p=128))
    t_bd = []
    for i, nm in enumerate(["bandc", "bandp", "bandc0", "bandp0"]):
        t_bd.append(c.dma(sp, f"cC6{i}", bands[:, i, :, :], W[nm].rearrange("g a b -> a g b")))
    wo_v = W["w_out"].rearrange("(k p) n -> p k n", p=128)
    t_wo = [c.dma(pool, f"cC7{i}", wout[:, 4 * i:4 * i + 4, :], wo_v[:, 4 * i:4 * i + 4, :]) for i in range(2)]
    lam_s, lam_e, neglam = S["lam_s"], S["lam_e"], S["neglam"]
    for i in range(2):
        t = op(dve, lambda e: e.tensor_tensor(out=lv[:, 2 * i, :], in0=lv[:, 2 * i, :], in1=lv[:, 2 * i + 1, :], op=ALU.mult),
               [t_lv[2 * i], t_lv[2 * i + 1]])
        t = op(dve, lambda e: e.tensor_reduce(out=lam_s[:, i:i + 1], in_=lv[:, 2 * i, :], axis=AX.X, op=ALU.add), [t])
    t = op(act, lambda e: e.activation(out=lam_e[:], in_=lam_s[:], func=AF.Exp), [t])
    t = op(dve, lambda e: e.tensor_tensor(out=neglam[:], in0=lam_e[:, 1:2], in1=lam_e[:, 0:1], op=ALU.subtract), [t])
    t_lam = op(dve, lambda e: e.tensor_tensor(out=neglam[:], in0=neglam[:], in1=lcb[:, 0:1], op=ALU.subtract), [t, t_lc])
    t_subg = op(dve, lambda e: e.tensor_scalar(out=subg[:], in0=subg[:], scalar1=lcb[:, 1:2], scalar2=None, op0=ALU.mult),
                [t_sg, t_lc])
    t_pwr = op(dve, lambda e: e.tensor_tensor(
        out=pw[:], in0=pwf[:],
        in1=psb[:].rearrange("p (g o d) -> p g o d", g=4, o=1).broadcast_to([128, 4, 2, 256]), op=ALU.mult),
        [t_pw, t_ps])

    ot, ub, sga, sgb = S["ot"], S["ub"], S["sga"], S["sgb"]
    od, m1, mb, mT, pT, sq = S["od"], S["m1"], S["mb"], S["mT"], S["pT"], S["sq"]
    ssh, rsh = S["ssh"], S["rsh"]
    Ov = I["Or"].rearrange("(b p) h v -> p b h v", p=128)
    Uv = I["U"].rearrange("(b p) d -> p b d", p=128)
    Av = I["sgA"].rearrange("(b p) d -> p b d", p=128)
    Bv = I["sgB"].rearrange("(b p) d -> p b d", p=128)
    ld = [None] * NB
    ot_free = [None, None]
    ub_free = [None, None, None]
    sga_free = [None, None]
    sgb_free = [None, None]
    t_halo = c.dma(sp, "ub2", ub[:, 2, :], I["Uh"], waits=[in_ready])

    def loads(tb):
        s2 = tb % 2
        ld[tb] = dict(
            o=c.dma(sp, f"ot{s2}", ot[:, s2, :, :], Ov[:, tb, :, :], waits=[in_ready, ot_free[s2]]),
            u=c.dma(sp, f"ub{tb % 3}", ub[:, tb % 3, :], Uv[:, tb, :], waits=[in_ready, ub_free[tb % 3]]),
            a=c.dma(sp, f"sga{s2}", sga[:, s2, :], Av[:, tb, :], waits=[in_ready, sga_free[s2]]),
            b=c.dma(sp, f"sgb{s2}", sgb[:, s2, :], Bv[:, tb, :], waits=[in_ready, sgb_free[s2]]))

    loads(0)
    nst = new_norm_state()
    nst["tpf"] = S.get("tpf0", [None, None])
    hn_tok = [None] * NB
    pTb_free = [None, None]
    yb_free = [None, None]
    wb_free = [None, None]
    pT_free = None
    mT_free = None
    mb_free = None
    m1_free = None
    od_free = None
    prev_u_tok = t_halo
    for tb in range(NB):
        if tb + 1 < NB:
            loads(tb + 1)
        s2 = tb % 2
        cur, prv = tb % 3, (tb - 1) % 3
        L = ld[tb]
        kc_, kp_ = (2, 3) if tb == 0 else (0, 1)
        pe.wait(L["u"], prev_u_tok, t_bd, pTb_free[0], pTb_free[1])
        for fc in range(8):
            g = fc // 2
            outp = c.ps[fc // 4][:, (fc % 4) * 128:(fc % 4 + 1) * 128]
            pe.eng.matmul(outp, lhsT=ub[:, cur, fc * 128:(fc + 1) * 128], rhs=bands[:, kc_, g, :], start=True, stop=False)
            ins = pe.eng.matmul(outp, lhsT=ub[:, prv, fc * 128:(fc + 1) * 128], rhs=bands[:, kp_, g, :], start=False, stop=True)
        t_pT = pe.sig(ins)
        ub_free[prv] = t_pT
        prev_u_tok = L["u"]
        act.wait(t_pT, pT_free)
        for hb in range(2):
            ins = act.eng.activation(out=pT[:, 4 * hb:4 * hb + 4, :], in_=c.ps[hb][:].rearrange("p (k t) -> p k t", k=4),
                                     func=AF.Copy)
        t_pTs = act.sig(ins)
        pTb_free = [t_pTs, t_pTs]
        pe.wait(t_pTs, t_pwr, yb_free[0], yb_free[1])
        for g in range(4):
            for cc in range(2):
                ins = pe.eng.matmul(c.ps[2 + g // 2][:, (g % 2) * 256:(g % 2 + 1) * 256], lhsT=pT[:, 2 * g + cc, :],
                                    rhs=pw[:, g, cc, :], start=(cc == 0), stop=(cc == 1))
        t_y = pe.sig(ins)
        pT_free = t_y
        dve.wait(t_y, L["a"], m1_free)
        for hb in range(2):
            ins = dve.eng.tensor_tensor(out=m1[:, hb * 512:(hb + 1) * 512], in0=c.ps[2 + hb][:],
                                        in1=sga[:, s2, hb * 512:(hb + 1) * 512], op=ALU.mult)
        t_m1 = dve.sig(ins)
        yb_free = [t_m1, t_m1]
        sga_free[s2] = t_m1
        o4 = ot[:, s2, :, :].rearrange("p (h c) v -> p h c v", c=2)
        t = op(dve, lambda e: e.scalar_tensor_tensor(out=od[:], in0=o4[:, :, 1, :], scalar=neglam[:, 0:1], in1=o4[:, :, 0, :],
                                                     op0=ALU.mult, op1=ALU.add), [L["o"], t_lam, od_free])
        ot_free[s2] = t
        sq3 = sq[:, 0, :].rearrange("p (h v) -> p h v", h=4)
        t = op(dve, lambda e: e.tensor_tensor(out=sq3, in0=od[:], in1=od[:], op=ALU.mult), [t, nst["sqf"][0]])
        t = op(dve, lambda e: e.tensor_reduce(out=ssh[:], in_=sq3, axis=AX.X, op=ALU.add), [t])
        nst["sqf"][0] = t
        t_r = rms_rstd(c, ssh[:], 256, nh[:, 0:4], rsh[:], [t, S["const_ready"]])
        dve.wait(t_r, t_subg)
        for h in range(4):
            ins = dve.eng.scalar_tensor_tensor(out=od[:, h, :], in0=od[:, h, :], scalar=rsh[:, h:h + 1], in1=subg[:],
                                               op0=ALU.mult, op1=ALU.mult)
        t = dve.sig(ins)
        odf = od[:].rearrange("p h v -> p (h v)")
        t = op(dve, lambda e: e.tensor_tensor(out=odf, in0=odf, in1=sgb[:, s2, :], op=ALU.mult), [t, L["b"]])
        sgb_free[s2] = t
        t_mb = op(dve, lambda e: e.tensor_tensor(out=mb[:], in0=odf, in1=m1[:], op=ALU.add), [t, t_m1, mb_free])
        m1_free = t_mb
        od_free = t_mb
        j = nst["i"] % 2
        tpb = c.ps[6 + j][:].bitcast(BF16)
        pe.wait(t_mb, nst["tpf"][j], S["const_ready"])
        for kc in range(8):
            ins = pe.eng.transpose(out=tpb[:, kc * 128:(kc + 1) * 128], in_=mb[:, kc * 128:(kc + 1) * 128],
                                   identity=S["ident"][:])
        t_tp = pe.sig(ins)
        mb_free = t_tp
        t_cp = op(act, lambda e: e.activation(out=mT[:], in_=tpb.rearrange("p (k t) -> p k t", k=8), func=AF.Copy),
                  [t_tp, mT_free])
        nst["tpf"][j] = t_cp
        nst["i"] += 1
        pe.wait(t_cp, t_wo, wb_free[0], wb_free[1])
        for hb in range(2):
            for kc in range(8):
                ins = pe.eng.matmul(c.ps[4 + hb][:], lhsT=mT[:, kc, :], rhs=wout[:, kc, hb * 512:(hb + 1) * 512],
                                    start=(kc == 0), stop=(kc == 7))
        t_w = pe.sig(ins)
        mT_free = t_w
        dve.wait(t_w, S.get("x_ready_all"))
        for hb in range(2):
            ins = dve.eng.tensor_tensor(out=X[:, tb, hb * 512:(hb + 1) * 512], in0=c.ps[4 + hb][:],
                                        in1=X[:, tb, hb * 512:(hb + 1) * 512], op=ALU.add)
        t_x = dve.sig(ins)
        wb_free = [t_x, t_x]
        hn_tok[tb] = norm_to_T(c, S, X[:, tb, :], g2b, t_g2, S["xnT"], tb, [t_x], nst)
    S["tpf0"] = nst["tpf"]
    return hn_tok


def phase_C2(c, X, W, S, hn_tok):
    pe, act, dve, pool, sp = c.pe, c.act, c.dve, c.pool, c.sp
    hnT = S["xnT"]
    wg, wu, w2, actT, sg = S["wg"], S["wu"], S["w2"], S["actT"], S["sg"]
    G = 4
    groups = [(c0, min(G, NFC - c0)) for c0 in range(0, NFC, G)]
    w1v = W["w_ffn_in"].rearrange("(k p) n -> p k n", p=128)
    w2v = W["w_ffn_out"].rearrange("(c p) d -> p c d", p=128)
    w_free = [None, None]
    w_tok = [None] * len(groups)

    def load_w(gi):
        c0, ng = groups[gi]
        s = gi % 2
        a = c.dma(pool, f"wg{s}", wg[:, s, :, 0:ng * 128], w1v[:, :, c0 * 128:(c0 + ng) * 128], waits=[w_free[s]])
        b = c.dma(pool, f"wu{s}", wu[:, s, :, 0:ng * 128], w1v[:, :, FH + c0 * 128:FH + (c0 + ng) * 128], waits=[w_free[s]])
        d = c.dma(pool, f"w2{s}", w2[:, s, 0:ng, :], w2v[:, c0:c0 + ng, :], waits=[w_free[s]])
        w_tok[gi] = [a, b, d]

    load_w(0)
    gu_free = [None, None]
    acc_free = None
    sg_free = [None, None]
    actT_free = [None, None]
    cnt = 0
    x_tok = [None] * NB
    last_tok = []
    for gi, (c0, ng) in enumerate(groups):
        if gi + 1 < len(groups):
            load_w(gi + 1)
        ws = gi % 2
        pend = None
        last_pe = None
        for tt in range(NB // 2 + 1):
            if tt < NB // 2:
                asl = tt % 2
                a_toks = []
                for ci in range(ng):
                    gb = cnt % 2
                    cnt += 1
                    bank = c.ps[gb]
                    pe.wait(hn_tok[2 * tt], hn_tok[2 * tt + 1], w_tok[gi], gu_free[gb])
                    for kc in range(8):
                        pe.eng.matmul(bank[:, 0:256], lhsT=wg[:, ws, kc, ci * 128:(ci + 1) * 128],
                                      rhs=hnT[:, kc, tt * 256:(tt + 1) * 256], start=(kc == 0), stop=(kc == 7))
                    for kc in range(8):
                        ins = pe.eng.matmul(bank[:, 256:512], lhsT=wu[:, ws, kc, ci * 128:(ci + 1) * 128],
                                            rhs=hnT[:, kc, tt * 256:(tt + 1) * 256], start=(kc == 0), stop=(kc == 7))
                    t_gu = pe.sig(ins)
                    t_s = op(act, lambda e: e.activation(out=sg[:, gb, :], in_=bank[:, 0:256], func=AF.Silu),
                             [t_gu, sg_free[gb]])
                    t_a = op(dve, lambda e: e.tensor_tensor(out=actT[:, asl, ci, :], in0=bank[:, 256:512], in1=sg[:, gb, :],
                                                            op=ALU.mult), [t_s, actT_free[asl] if ci == 0 else None])
                    gu_free[gb] = t_a
                    sg_free[gb] = t_a
                    a_toks.append(t_a)
                cur = (tt, asl, a_toks)
            else:
                cur = None
            if pend is not None:
                ptt, pasl, ptoks = pend
                pe.wait(ptoks, acc_free)
                for blk in range(2):
                    for hb in range(2):
                        for ci in range(ng):
                            ins = pe.eng.matmul(c.ps[2 + blk * 2 + hb][:], lhsT=actT[:, pasl, ci, blk * 128:(blk + 1) * 128],
                                                rhs=w2[:, ws, ci, hb * 512:(hb + 1) * 512], start=(ci == 0), stop=(ci == ng - 1))
                t_d = pe.sig(ins)
                last_pe = t_d
                actT_free[pasl] = t_d
                dve.wait(t_d)
                for blk in range(2):
                    for hb in range(2):
                        tbk = 2 * ptt + blk
                        ins = dve.eng.tensor_tensor(out=X[:, tbk, hb * 512:(hb + 1) * 512], in0=c.ps[2 + blk * 2 + hb][:],
                                                    in1=X[:, tbk, hb * 512:(hb + 1) * 512], op=ALU.add)
                    x_tok[2 * ptt + blk] = None
                t_x = dve.sig(ins)
                acc_free = t_x
                x_tok[2 * ptt] = t_x
                x_tok[2 * ptt + 1] = t_x
            pend = cur
        w_free[ws] = last_pe
    return x_tok


def alloc_C1(c):
    S = {}
    S["g2b"] = c.sb("g2b", [128, D], F32)
    S["subg"] = c.sb("subg", [128, 256], F32)
    S["psb"] = c.sb("psb", [128, D], F32)
    S["lcb"] = c.sb("lcb", [128, 2], F32)
    S["lv"] = c.sb("lv", [128, 4, 128], F32)
    S["lam_s"] = c.sb("lam_s", [128, 2], F32)
    S["lam_e"] = c.sb("lam_e", [128, 2], F32)
    S["neglam"] = c.sb("neglam", [128, 1], F32)
    S["wout"] = c.sb("wout", [128, 8, D], BF16)
    S["pwf"] = c.sb("pwf", [128, 4, 2, 256], F32)
    S["pw"] = c.sb("pw", [128, 4, 2, 256], BF16)
    S["bands"] = c.sb("bands", [128, 4, 4, 128], BF16)
    S["ot"] = c.sb("ot", [128, 2, 8, 256], F32)
    S["ub"] = c.sb("ub", [128, 3, D], BF16)
    S["sga"] = c.sb("sga", [128, 2, D], BF16)
    S["sgb"] = c.sb("sgb", [128, 2, D], BF16)
    S["od"] = c.sb("od", [128, 4, 256], F32)
    S["m1"] = c.sb("m1", [128, D], F32)
    S["mb"] = c.sb("mb", [128, D], BF16)
    S["mT"] = c.sb("mT", [128, 8, 128], BF16)
    S["pT"] = c.sb("pT", [128, 8, 128], BF16)
    S["ssh"] = c.sb("ssh", [128, 4], F32)
    S["rsh"] = c.sb("rsh", [128, 4], F32)
    return S


def alloc_C2(c):
    S = {}
    S["wg"] = c.sb("wg", [128, 2, 8, 512], BF16)
    S["wu"] = c.sb("wu", [128, 2, 8, 512], BF16)
    S["w2"] = c.sb("w2", [128, 2, 4, D], BF16)
    S["actT"] = c.sb("actT", [128, 2, 4, 256], BF16)
    S["sg"] = c.sb("sgf", [128, 2, 256], F32)
    return S


def alloc_norm(c):
    S = {}
    S["sq"] = c.sb("sq", [128, 2, D], F32)
    S["ss"] = c.sb("ss", [128, 2], F32)
    S["rs"] = c.sb("rs", [128, 2], F32)
    S["xnb"] = c.sb("xnb", [128, 2, D], BF16)
    S["xnT"] = c.sb("xnT", [128, 8, TPC], BF16)
    return S


def build_C():
    nc = bass.Bass("TRN2", target_bir_lowering=False)
    x = dram(nc, "x", [TPC, D], F32, "ExternalInput")
    xo = dram(nc, "xo", [TPC, D], F32, "ExternalOutput")
    W = {}
    for nm, shp in [("norm2_g", [D]), ("subln_g", [256]), ("pool_scale", [D]), ("lc", [2]), ("lam_q1", [128]),
                    ("lam_k1", [128]), ("lam_q2", [128]), ("lam_k2", [128]), ("pool_w", [4, 256, 256]),
                    ("w_out", [D, D]), ("w_ffn_in", [D, 2 * FH]), ("w_ffn_out", [FH, D])]:
        W[nm] = dram(nc, nm, shp, F32, "ExternalInput")
    for nm in ["bandc", "bandp", "bandc0", "bandp0"]:
        W[nm] = dram(nc, nm, [4, 128, 128], BF16, "ExternalInput")
    K = dict(ident=dram(nc, "ident", [128, 128], BF16, "ExternalInput"))
    I = dict(Or=dram(nc, "Or", [TPC, 8, 256], F32, "ExternalInput"),
             U=dram(nc, "U", [TPC, D], BF16, "ExternalInput"),
             Uh=dram(nc, "Uh", [128, D], BF16, "ExternalInput"),
             sgA=dram(nc, "sgA", [TPC, D], BF16, "ExternalInput"),
             sgB=dram(nc, "sgB", [TPC, D], BF16, "ExternalInput"))
    with ExitStack() as es:
        c = Ctx(nc, es)
        es.enter_context(nc.Block())
        S = {}
        S["ident"] = c.sb("ident", [128, 128], BF16)
        S["neghalf"] = c.sb("neghalf", [128, 16], F32)
        S["const_ready"] = [c.dma(c.sp, "k0", S["ident"][:], K["ident"]),
                            c.pool.sig(c.pool.eng.memset(S["neghalf"][:], -0.5))]
        S.update(alloc_norm(c))
        X = c.sb("X", [128, NB, D], F32)
        xr = load_X(c, X, x)
        S["x_ready_all"] = xr
        with ExitStack() as es1:
            c.es = es1
            S.update(alloc_C1(c))
            hn_tok = phase_C1(c, X, W, I, S)
            last = [(q.sem, q.n, q.name) for q in (c.pe, c.act, c.dve, c.pool) if q.n > 0]
            for q in (c.pe, c.act, c.dve, c.pool, c.sp):
                q.wait(*last)
        c.es = es
        S.update(alloc_C2(c))
        x_tok = phase_C2(c, X, W, S, hn_tok)
        xov = xo.rearrange("(b p) d -> p b d", p=128)
        toks = []
        for i in range(4):
            toks.append(c.dma(c.sp, f"xs{i}", xov[:, 4 * i:4 * i + 4, :], X[:, 4 * i:4 * i + 4, :],
                              waits=[x_tok[4 * i + j] for j in range(4)]))
        c.sp.wait(*toks)
    return nc


def host_consts():
    pos = np.arange(SEQ, dtype=np.float32)
    inv_freq = (np.float32(500000.0) ** (-np.arange(0, ROT, 2, dtype=np.float32) / np.float32(ROT))).astype(np.float32)
    ang = (pos[:, None] * inv_freq[None, :]).astype(np.float32)
    cos = np.cos(ang).astype(np.float32)
    sin = np.sin(ang).astype(np.float32)
    ident = np.eye(128, dtype=np.float32).astype(ml_dtypes.bfloat16)
    bandc = np.zeros((4, 128, 128), np.float32)
    bandp = np.zeros((4, 128, 128), np.float32)
    bandc0 = np.zeros((4, 128, 128), np.float32)
    for gi, w in enumerate(POOL_WINDOWS):
        for t in range(128):
            for j in range(w):
                tp = t - j
                if tp >= 0:
                    bandc[gi, tp, t] += 1.0 / w
                    bandc0[gi, tp, t] += 1.0 / min(t + 1, w)
                else:
                    bandp[gi, tp + 128, t] += 1.0 / w
            bandc[gi, t, t] -= 1.0
            bandc0[gi, t, t] -= 1.0
    bf = ml_dtypes.bfloat16
    return dict(cos=cos, sin=sin, ident=ident, bandc=bandc.astype(bf), bandp=bandp.astype(bf),
                bandc0=bandc0.astype(bf), bandz=np.zeros((4, 128, 128), bf))


_PROGS = {}


def _prog(name, fn):
    if name not in _PROGS:
        _PROGS[name] = fn()
    return _PROGS[name]


def _run(nc, maps):
    res = run_bass_kernel_spmd(nc, maps, core_ids=list(range(NCORES)))
    return res.results


def kernel_unfused(**inp):
    bf = ml_dtypes.bfloat16
    f32 = np.float32
    hc = host_consts()
    x = np.ascontiguousarray(np.asarray(inp["x"], f32)[0])
    g = lambda k, l: np.ascontiguousarray(np.asarray(inp[k], f32)[l])
    xs = [np.ascontiguousarray(x[c * TPC:(c + 1) * TPC]) for c in range(NCORES)]
    zeros_h = np.zeros((128, D), bf)
    for l in range(DEPTH):
        lam_init = 0.8 - 0.6 * math.exp(-0.3 * l)
        mapsA = []
        for c in range(NCORES):
            sl = slice(c * TPC, (c + 1) * TPC)
            mapsA.append(dict(x=xs[c], norm1_g=g("norm1_g", l), q_norm_g=g("q_norm_g", l), k_norm_g=g("k_norm_g", l),
                              w_in=g("w_in", l), ident=hc["ident"], cos=np.ascontiguousarray(hc["cos"][sl]),
                              sin=np.ascontiguousarray(hc["sin"][sl])))
        ra = _run(_prog("A", build_A), mapsA)
        mapsB = []
        for p in range(NCORES):
            QT = np.concatenate([np.asarray(ra[r]["QT"])[p] for r in range(NCORES)], axis=1)
            KT = np.concatenate([np.asarray(ra[r]["KT"])[p] for r in range(NCORES)], axis=1)
            V = np.concatenate([np.asarray(ra[r]["V"])[p // 2] for r in range(NCORES)], axis=0)
            mapsB.append(dict(QT=np.ascontiguousarray(QT), KT=np.ascontiguousarray(KT), V=np.ascontiguousarray(V)))
        rb = _run(_prog("B", build_B), mapsB)
        mapsC = []
        for c in range(NCORES):
            sl = slice(c * TPC, (c + 1) * TPC)
            Or = np.stack([np.asarray(rb[p]["O"])[sl] for p in range(NCORES)], axis=1)
            Uh = np.asarray(ra[c - 1]["U"])[-128:] if c > 0 else zeros_h
            mapsC.append(dict(
                x=xs[c], Or=np.ascontiguousarray(Or), U=np.asarray(ra[c]["U"]), Uh=np.ascontiguousarray(Uh),
                sgA=np.asarray(ra[c]["sgA"]), sgB=np.asarray(ra[c]["sgB"]),
                norm2_g=g("norm2_g", l), subln_g=g("subln_g", l), pool_scale=g("pool_scale", l),
                lc=np.array([lam_init, 1.0 - lam_init], f32),
                lam_q1=g("lam_q1", l), lam_k1=g("lam_k1", l), lam_q2=g("lam_q2", l), lam_k2=g("lam_k2", l),
                pool_w=g("pool_w", l), w_out=g("w_out", l), w_ffn_in=g("w_ffn_in", l), w_ffn_out=g("w_ffn_out", l),
                ident=hc["ident"], bandc=hc["bandc"], bandp=hc["bandp"],
                bandc0=(hc["bandc0"] if c == 0 else hc["bandc"]), bandp0=(hc["bandz"] if c == 0 else hc["bandp"])))
        rc = _run(_prog("C", build_C), mapsC)
        xs = [np.ascontiguousarray(np.asarray(rc[c]["xo"], f32)) for c in range(NCORES)]
    return np.concatenate(xs, axis=0)[None].astype(f32)


def kernel(**inputs):
    return kernel_unfused(**inputs)
```
